# Optimizing a Trainium2 kernel written in Bass

```python
import math
import jax, jax.numpy as jnp
from jax import lax
import numpy as np

D_MODEL = 4096
BATCH = 1
SEQ = 16384
DEPTH = 2

HEAD_DIM = 128
ROT_DIM = HEAD_DIM // 4
ROPE_THETA = 500000.0
MOBA_HEADS = D_MODEL // (2 * HEAD_DIM)
MOBA_WIDTH = MOBA_HEADS * HEAD_DIM
MOBA_BLOCK = 256
MOBA_TOPK = 3
MOBA_QCHUNK = 32
CONV_CH = D_MODEL // 2
CONV_WIDTH = 31
DIL_CONFIGS = ((128, 1), (512, 4), (2048, 16))
DIL_HEADS = D_MODEL // (2 * HEAD_DIM)
DIL_WIDTH = DIL_HEADS * HEAD_DIM
D_FF = 4 * D_MODEL
ALPHA = (2 * DEPTH) ** 0.25
BETA = (8 * DEPTH) ** -0.25
LN_EPS = 1e-5
N_EVEN = (DEPTH + 1) // 2
N_ODD = DEPTH // 2
W_AB_IN = 3 * MOBA_WIDTH + 2 * CONV_CH
W_C_IN = len(DIL_CONFIGS) * 3 * DIL_WIDTH

kernel_name = 'hybrid_moba_conformer_dilated_deepnorm'


def _layernorm(x, g, b):
    xf = x.astype(jnp.float32)
    mu = jnp.mean(xf, axis=-1, keepdims=True)
    var = jnp.mean(jnp.square(xf - mu), axis=-1, keepdims=True)
    y = (xf - mu) * lax.rsqrt(var + LN_EPS) * g.astype(jnp.float32) + b.astype(jnp.float32)
    return y.astype(x.dtype)


def _rope_tables(seq):
    pos = jnp.arange(seq, dtype=jnp.float32)
    inv = ROPE_THETA ** (-jnp.arange(0, ROT_DIM, 2, dtype=jnp.float32) / ROT_DIM)
    ang = pos[:, None] * inv[None, :]
    return jnp.cos(ang), jnp.sin(ang)


def _apply_partial_rope(x, cos, sin):
    half = ROT_DIM // 2
    shape = (x.shape[1],) + (1,) * (x.ndim - 3) + (half,)
    cos = cos.reshape(shape).astype(x.dtype)
    sin = sin.reshape(shape).astype(x.dtype)
    x1 = x[..., :half]
    x2 = x[..., half:ROT_DIM]
    return jnp.concatenate([x1 * cos - x2 * sin, x2 * cos + x1 * sin, x[..., ROT_DIM:]], axis=-1)


def _moba_attention(q, k, v):
    b, h, s, dh = q.shape
    nb = -(-s // MOBA_BLOCK)
    s_pad = nb * MOBA_BLOCK
    pad = ((0, 0), (0, 0), (0, s_pad - s), (0, 0))
    q = jnp.pad(q, pad)
    k = jnp.pad(k, pad)
    v = jnp.pad(v, pad)
    kb = k.reshape(b, h, nb, MOBA_BLOCK, dh)
    vb = v.reshape(b, h, nb, MOBA_BLOCK, dh)
    k_mean = jnp.mean(kb.astype(jnp.float32), axis=3)
    gate = jnp.einsum('bhsd,bhnd->bhsn', q.astype(jnp.float32), k_mean)
    q_blk = jnp.arange(s_pad) // MOBA_BLOCK
    fully_past = jnp.arange(nb)[None, :] < q_blk[:, None]
    gate = jnp.where(fully_past, gate, -jnp.inf)
    n_sel = min(MOBA_TOPK, nb)
    _, sel_idx = lax.top_k(gate, n_sel)
    sel_valid = jnp.arange(n_sel)[None, :] < q_blk[:, None]
    n_chunks = s_pad // MOBA_QCHUNK
    scale = dh ** -0.5
    q_c = q.reshape(b, h, n_chunks, MOBA_QCHUNK, dh).transpose(2, 0, 1, 3, 4)
    idx_c = sel_idx.reshape(b, h, n_chunks, MOBA_QCHUNK, n_sel).transpose(2, 0, 1, 3, 4)
    valid_c = sel_valid.reshape(n_chunks, MOBA_QCHUNK, n_sel)
    b_ix = jnp.arange(b)[:, None, None, None]
    h_ix = jnp.arange(h)[None, :, None, None]

    def chunk(args):
        qc, idx, valid, ci = args
        k_sel = kb[b_ix, h_ix, idx]
        v_sel = vb[b_ix, h_ix, idx]
        s_sel = jnp.einsum('bhqd,bhqjkd->bhqjk', qc, k_sel).astype(jnp.float32) * scale
        s_sel = jnp.where(valid[None, None, :, :, None], s_sel, -jnp.inf)
        blk = (ci * MOBA_QCHUNK) // MOBA_BLOCK
        k_own = lax.dynamic_index_in_dim(kb, blk, axis=2, keepdims=False)
        v_own = lax.dynamic_index_in_dim(vb, blk, axis=2, keepdims=False)
        s_own = jnp.einsum('bhqd,bhkd->bhqk', qc, k_own).astype(jnp.float32) * scale
        q_pos = ci * MOBA_QCHUNK + jnp.arange(MOBA_QCHUNK)
        k_pos = blk * MOBA_BLOCK + jnp.arange(MOBA_BLOCK)
        s_own = jnp.where(k_pos[None, :] <= q_pos[:, None], s_own, -jnp.inf)
        sc = jnp.concatenate([s_sel.reshape(b, h, MOBA_QCHUNK, n_sel * MOBA_BLOCK), s_own], axis=-1)
        p = jax.nn.softmax(sc, axis=-1).astype(v.dtype)
        p_sel = p[..., :n_sel * MOBA_BLOCK].reshape(b, h, MOBA_QCHUNK, n_sel, MOBA_BLOCK)
        p_own = p[..., n_sel * MOBA_BLOCK:]
        return (jnp.einsum('bhqjk,bhqjkd->bhqd', p_sel, v_sel)
                + jnp.einsum('bhqk,bhkd->bhqd', p_own, v_own))

    out = lax.map(chunk, (q_c, idx_c, valid_c, jnp.arange(n_chunks)))
    out = out.transpose(1, 2, 0, 3, 4).reshape(b, h, s_pad, dh)
    return out[:, :, :s]


def _banded_window_attention(q, k, v, window):
    lead = tuple(q.shape[:-2])
    L, dh = q.shape[-2], q.shape[-1]
    blk = window
    nblk = -(-L // blk)
    L_pad = nblk * blk
    nl = len(lead)
    q = jnp.pad(q, [(0, 0)] * nl + [(0, L_pad - L), (0, 0)])
    k = jnp.pad(k, [(0, 0)] * nl + [(blk, L_pad - L), (0, 0)])
    v = jnp.pad(v, [(0, 0)] * nl + [(blk, L_pad - L), (0, 0)])
    qb = q.reshape(lead + (nblk, blk, dh))
    kb = k.reshape(lead + (nblk + 1, blk, dh))
    vb = v.reshape(lead + (nblk + 1, blk, dh))
    k_slab = jnp.concatenate([kb[..., :-1, :, :], kb[..., 1:, :, :]], axis=-2)
    v_slab = jnp.concatenate([vb[..., :-1, :, :], vb[..., 1:, :, :]], axis=-2)
    s = jnp.einsum('...nqd,...nkd->...nqk', qb, k_slab).astype(jnp.float32) * (dh ** -0.5)
    q_pos = (jnp.arange(nblk) * blk)[:, None] + jnp.arange(blk)[None, :]
    k_pos = (jnp.arange(nblk) * blk - blk)[:, None] + jnp.arange(2 * blk)[None, :]
    dist = q_pos[:, :, None] - k_pos[:, None, :]
    mask = (dist >= 0) & (dist <= window) & (k_pos[:, None, :] >= 0)
    s = jnp.where(mask, s, -jnp.inf)
    m = jnp.max(s, axis=-1, keepdims=True)
    p = jnp.exp(s - m)
    l = jnp.sum(p, axis=-1, keepdims=True)
    o = jnp.einsum('...nqk,...nkd->...nqd', p.astype(v.dtype), v_slab).astype(jnp.float32) / l
    lse = (m + jnp.log(l))[..., 0]
    o = o.reshape(lead + (L_pad, dh))[..., :L, :]
    lse = lse.reshape(lead + (L_pad,))[..., :L]
    return o, lse


def _dilated_attention(q, k, v, window, dil):
    b, g, s, dh = q.shape
    L = s // dil

    def split(t):
        return t.reshape(b, g, L, dil, dh).transpose(0, 1, 3, 2, 4)

    o, lse = _banded_window_attention(split(q), split(k), split(v), window // dil)
    o = o.transpose(0, 1, 3, 2, 4).reshape(b, g, s, dh)
    lse = lse.transpose(0, 1, 3, 2).reshape(b, g, s)
    return o, lse


def _moba_conv_mixer(h, w_in, w_out, conv_w, conv_b, conv_ln_g, conv_ln_b, cos, sin):
    b, s, _ = h.shape
    proj = h @ w_in
    qkv = proj[..., :3 * MOBA_WIDTH].reshape(b, s, 3, MOBA_HEADS, HEAD_DIM)
    q = _apply_partial_rope(qkv[:, :, 0], cos, sin).transpose(0, 2, 1, 3)
    k = _apply_partial_rope(qkv[:, :, 1], cos, sin).transpose(0, 2, 1, 3)
    v = qkv[:, :, 2].transpose(0, 2, 1, 3)
    a_out = _moba_attention(q, k, v).transpose(0, 2, 1, 3).reshape(b, s, MOBA_WIDTH)
    glu = proj[..., 3 * MOBA_WIDTH:]
    u = glu[..., :CONV_CH] * jax.nn.sigmoid(glu[..., CONV_CH:])
    u = lax.conv_general_dilated(u, conv_w[:, None, :], window_strides=(1,),
                                 padding=((CONV_WIDTH - 1, 0),),
                                 dimension_numbers=('NWC', 'WIO', 'NWC'),
                                 feature_group_count=CONV_CH) + conv_b
    u = jax.nn.silu(_layernorm(u, conv_ln_g, conv_ln_b))
    return jnp.concatenate([a_out, u], axis=-1) @ w_out


def _dilated_mixer(h, w_in, w_out, cos, sin):
    b, s, _ = h.shape
    n_g = len(DIL_CONFIGS)
    qkv = (h @ w_in).reshape(b, s, n_g, 3, DIL_HEADS, HEAD_DIM)
    q = _apply_partial_rope(qkv[:, :, :, 0], cos, sin)
    k = _apply_partial_rope(qkv[:, :, :, 1], cos, sin)
    v = qkv[:, :, :, 2]
    outs = []
    lses = []
    for gi, (window, dil) in enumerate(DIL_CONFIGS):
        o, lse = _dilated_attention(q[:, :, gi].transpose(0, 2, 1, 3), k[:, :, gi].transpose(0, 2, 1, 3),
                                    v[:, :, gi].transpose(0, 2, 1, 3), window, dil)
        outs.append(o)
        lses.append(lse)
    wts = jax.nn.softmax(jnp.stack(lses), axis=0)
    o = jnp.sum(wts[..., None] * jnp.stack(outs), axis=0)
    o = o.astype(h.dtype).transpose(0, 2, 1, 3).reshape(b, s, DIL_WIDTH)
    return o @ w_out


def _sq_relu_mlp(h, w1, w2):
    return jnp.square(jax.nn.relu(h @ w1)) @ w2


def setup_inputs(seed: int = 0) -> dict:
    key = jax.random.key(seed)
    ks = jax.random.split(key, 16)

    def nrm(k, shape, std):
        return jax.random.normal(k, shape, jnp.float32) * std

    col_ab = jnp.concatenate([jnp.ones((2 * MOBA_WIDTH,), jnp.float32),
                              jnp.full((MOBA_WIDTH,), BETA, jnp.float32),
                              jnp.ones((2 * CONV_CH,), jnp.float32)])
    col_c = jnp.ones((len(DIL_CONFIGS), 3, DIL_WIDTH), jnp.float32).at[:, 2].set(BETA).reshape(-1)
    return {
        'x': jax.random.normal(ks[0], (BATCH, SEQ, D_MODEL), jnp.float32),
        'c': jax.random.normal(ks[1], (BATCH, D_MODEL), jnp.float32),
        'w_ada': nrm(ks[2], (DEPTH, D_MODEL, 6 * D_MODEL), 0.1 * D_MODEL ** -0.5),
        'b_ada': nrm(ks[3], (DEPTH, 6 * D_MODEL), 0.01),
        'w_in_ab': nrm(ks[4], (N_EVEN, D_MODEL, W_AB_IN), D_MODEL ** -0.5) * col_ab,
        'w_out_ab': nrm(ks[5], (N_EVEN, MOBA_WIDTH + CONV_CH, D_MODEL), BETA * (MOBA_WIDTH + CONV_CH) ** -0.5),
        'conv_w': nrm(ks[6], (N_EVEN, CONV_WIDTH, CONV_CH), CONV_WIDTH ** -0.5),
        'conv_b': nrm(ks[7], (N_EVEN, CONV_CH), 0.01),
        'conv_ln_g': 1.0 + nrm(ks[8], (N_EVEN, CONV_CH), 0.01),
        'conv_ln_b': nrm(ks[9], (N_EVEN, CONV_CH), 0.01),
        'w_in_c': nrm(ks[10], (N_ODD, D_MODEL, W_C_IN), D_MODEL ** -0.5) * col_c,
        'w_out_c': nrm(ks[11], (N_ODD, DIL_WIDTH, D_MODEL), BETA * DIL_WIDTH ** -0.5),
        'w_ff1': nrm(ks[12], (DEPTH, D_MODEL, D_FF), BETA * D_MODEL ** -0.5),
        'w_ff2': nrm(ks[13], (DEPTH, D_FF, D_MODEL), BETA * D_FF ** -0.5),
        'ln_g': 1.0 + nrm(ks[14], (DEPTH, 2, D_MODEL), 0.01),
        'ln_b': nrm(ks[15], (DEPTH, 2, D_MODEL), 0.01),
    }


def reference(x, c, w_ada, b_ada, w_in_ab, w_out_ab, conv_w, conv_b, conv_ln_g, conv_ln_b,
              w_in_c, w_out_c, w_ff1, w_ff2, ln_g, ln_b):
    b, s, d = x.shape
    cos, sin = _rope_tables(s)
    cond = jax.nn.silu(c)
    for i in range(DEPTH):
        mod = (cond @ w_ada[i] + b_ada[i]).reshape(b, 6, d)[:, :, None, :]
        shift1, scale1, gate1 = mod[:, 0], mod[:, 1], mod[:, 2]
        shift2, scale2, gate2 = mod[:, 3], mod[:, 4], mod[:, 5]
        h = x * (1.0 + scale1) + shift1
        if i % 2 == 0:
            j = i // 2
            y = _moba_conv_mixer(h, w_in_ab[j], w_out_ab[j], conv_w[j], conv_b[j],
                                 conv_ln_g[j], conv_ln_b[j], cos, sin)
        else:
            j = i // 2
            y = _dilated_mixer(h, w_in_c[j], w_out_c[j], cos, sin)
        x = _layernorm(ALPHA * x + (1.0 + gate1) * y, ln_g[i, 0], ln_b[i, 0])
        h = x * (1.0 + scale2) + shift2
        x = _layernorm(ALPHA * x + (1.0 + gate2) * _sq_relu_mlp(h, w_ff1[i], w_ff2[i]),
                       ln_g[i, 1], ln_b[i, 1])
    return x
```

```python
import contextlib
import numpy as np
import ml_dtypes
import concourse.bass as bass
import concourse.mybir as mybir
from concourse.bass_utils import run_bass_kernel_spmd

BF = ml_dtypes.bfloat16
F32 = mybir.dt.float32
BF16 = mybir.dt.bfloat16
AF = mybir.ActivationFunctionType
ALU = mybir.AluOpType

SAME_ENGINE_RAW = True


class Prog:
    ENGS = ("pe", "act", "dve", "pool", "sp")

    def __init__(self, nc, es):
        self.nc = nc
        self.es = es
        self.ops = {e: [] for e in self.ENGS}
        self.bufs = {}
        self.dma_cnt = {}
        self.nsb = 0
        self.psum_ids = set()

    def sb(self, name, shape, dt):
        return self.es.enter_context(self.nc.sbuf_tensor(name, list(shape), dt))

    def ps(self, name, shape, dt=F32, ids=()):
        self.psum_ids.update(ids)
        return self.es.enter_context(self.nc.psum_tensor(name, list(shape), dt))

    def _deps(self, eng, reads, writes, is_dma):
        deps = []
        for b in reads:
            st = self.bufs.setdefault(b, [[], []])
            for ev in st[0]:
                deps.append(ev)
            if (b[0] if isinstance(b, tuple) else b) in self.psum_ids:
                for ev in st[1]:
                    if not (ev[0] == "eng" and ev[1] == eng):
                        deps.append(ev)
        for b in writes:
            st = self.bufs.setdefault(b, [[], []])
            for ev in st[1]:
                deps.append(ev)
            for ev in st[0]:
                deps.append(ev)
        return deps

    def op(self, eng, fn, reads=(), writes=(), dma_key=None):
        reads = list(reads)
        writes = list(writes)
        is_dma = dma_key is not None
        deps = self._deps(eng, reads, writes, is_dma)
        lst = self.ops[eng]
        idx = len(lst)
        if is_dma:
            c = self.dma_cnt.get(dma_key, 0) + 1
            self.dma_cnt[dma_key] = c
            ev = ("dma", dma_key, c)
        else:
            ev = ("eng", eng, idx)
        fdeps = []
        for d in deps:
            if d[0] == "eng" and d[1] == eng and not is_dma:
                continue
            if is_dma and d[0] == "dma" and d[1] == dma_key:
                continue
            fdeps.append(d)
        if SAME_ENGINE_RAW and not is_dma and eng in ("act", "dve", "pool"):
            for b in reads:
                for d in self.bufs[b][0]:
                    if d[0] == "eng" and d[1] == eng:
                        fdeps.append(d)
        rec = dict(fn=fn, deps=fdeps, ev=ev, sig=False)
        lst.append(rec)
        for b in reads:
            st = self.bufs[b]
            st[1] = [w for w in st[1] if not (w[0] == ev[0] and w[1] == ev[1])] + [ev]
        for b in writes:
            st = self.bufs[b]
            if st[1]:
                st[0] = [ev]
                st[1] = []
            else:
                st[0] = [w for w in st[0] if not (w[0] == ev[0] and w[1] == ev[1])] + [ev]
        return ev

    def build(self, final_waits=()):
        nc = self.nc
        needed = {e: set() for e in self.ENGS}
        for e in self.ENGS:
            for rec in self.ops[e]:
                for d in rec["deps"]:
                    if d[0] == "eng":
                        needed[d[1]].add(d[2])
        count_of = {e: {} for e in self.ENGS}
        for e in self.ENGS:
            c = 0
            for i, rec in enumerate(self.ops[e]):
                if i in needed[e] and rec["ev"][0] == "eng":
                    c += 1
                    rec["sig"] = True
                    count_of[e][i] = c
        esem = {e: self.es.enter_context(nc.semaphore("s_" + e)) for e in self.ENGS}
        dsem = {}
        for k in self.dma_cnt:
            dsem[k] = self.es.enter_context(nc.semaphore("d_%d" % len(dsem)))
        self.nsem = len(esem) + len(dsem)

        def replay(engname, eobj):
            waited = {}
            for rec in self.ops[engname]:
                for d in rec["deps"]:
                    if d[0] == "eng":
                        key = ("e", d[1])
                        val = count_of[d[1]][d[2]]
                        sem = esem[d[1]]
                    else:
                        key = ("d", d[1])
                        val = d[2] * 16
                        sem = dsem[d[1]]
                    if waited.get(key, 0) >= val:
                        continue
                    waited[key] = val
                    eobj.wait_ge(sem, val)
                ins = rec["fn"](eobj)
                if rec["ev"][0] == "dma":
                    ins.then_inc(dsem[rec["ev"][1]], 16)
                elif rec["sig"]:
                    ins.then_inc(esem[engname], 1)
            if engname == "sp":
                for k, c in self.dma_cnt.items():
                    eobj.wait_ge(dsem[k], c * 16)

        with nc.Block() as block:
            @block.tensor
            def _(t):
                replay("pe", t)

            @block.scalar
            def _(a):
                replay("act", a)

            @block.vector
            def _(v):
                replay("dve", v)

            @block.gpsimd
            def _(g):
                replay("pool", g)

            @block.sync
            def _(s):
                replay("sp", s)


ALPHA = 4 ** 0.25
LN_EPS = 1e-5


def ln_block(P, nm, Z, DC, TM, PS_ST, ONESD, T, post, eps=None, zname="Z"):
    SQ, MEAN, MSQ, VAR, RSTD, NMR = T["SQ"], T["MEAN"], T["MSQ"], T["VAR"], T["RSTD"], T["NMR"]
    for m in range(DC):
        s = m % 2
        P.op("act", lambda e, m=m, s=s: e.activation(out=SQ[s][:], in_=Z[:, m, :], func=AF.Square),
             reads=[(zname, m)], writes=[("SQ", s)])
        P.op("pe", lambda e, m=m: e.matmul(PS_ST[0][:], lhsT=ONESD[:], rhs=Z[:, m, :], start=(m == 0), stop=(m == DC - 1)),
             reads=[(zname, m), "ONESD"], writes=[("ST", 0)])
        P.op("pe", lambda e, m=m, s=s: e.matmul(PS_ST[1][:], lhsT=ONESD[:], rhs=SQ[s][:], start=(m == 0), stop=(m == DC - 1)),
             reads=[("SQ", s), "ONESD"], writes=[("ST", 1)])
    if eps is None:
        eps = LN_EPS / (ALPHA * ALPHA)
    P.op("dve", lambda e: e.tensor_copy(MEAN[:], PS_ST[0][:]), reads=[("ST", 0)], writes=["MEAN"])
    P.op("dve", lambda e: e.tensor_tensor(MSQ[:], MEAN[:], MEAN[:], ALU.mult), reads=["MEAN"], writes=["MSQ"])
    P.op("dve", lambda e: e.tensor_tensor(VAR[:], PS_ST[1][:], MSQ[:], ALU.subtract), reads=[("ST", 1), "MSQ"], writes=["VAR"])
    P.op("dve", lambda e: e.tensor_scalar(VAR[:], VAR[:], eps, None, ALU.add), reads=["VAR"], writes=["VAR"])
    P.op("act", lambda e: e.activation(out=MSQ[:], in_=VAR[:], func=AF.Sqrt), reads=["VAR"], writes=["MSQ"])
    P.op("dve", lambda e: e.reciprocal(RSTD[:], MSQ[:]), reads=["MSQ"], writes=["RSTD"])
    P.op("dve", lambda e: e.scalar_tensor_tensor(NMR[:], MEAN[:], -1.0, RSTD[:], ALU.mult, ALU.mult),
         reads=["MEAN", "RSTD"], writes=["NMR"])
    for m in range(DC):
        P.op("dve", lambda e, m=m: e.tensor_tensor(Z[:, m, :], Z[:, m, :], RSTD[:], ALU.mult),
             reads=[(zname, m), "RSTD"], writes=[(zname, m)])
        P.op("pool", lambda e, m=m: e.tensor_tensor(Z[:, m, :], Z[:, m, :], NMR[:], ALU.add),
             reads=[(zname, m), "NMR"], writes=[(zname, m)])
        post(m)


def build_back(D, DFF, KA, TOK, TM=512, HG=2):
    DC, KC, HC = D // 128, KA // 128, DFF // 128
    RC = max(DC, KC)
    NT = TOK // TM
    nc = bass.Bass("TRN2", target_bir_lowering=False)
    aT = nc.dram_tensor("aT", [KA, TOK], BF16, kind="ExternalInput").ap()
    xT = nc.dram_tensor("xT", [D, TOK], F32, kind="ExternalInput").ap()
    modT = nc.dram_tensor("modT", [128, 6, DC], F32, kind="ExternalInput").ap()
    lnv = nc.dram_tensor("lnv", [128, 4, DC], F32, kind="ExternalInput").ap()
    w_out = nc.dram_tensor("w_out", [KA, D], F32, kind="ExternalInput").ap()
    w_ff1 = nc.dram_tensor("w_ff1", [D, DFF], F32, kind="ExternalInput").ap()
    w_ff2 = nc.dram_tensor("w_ff2", [DFF, D], F32, kind="ExternalInput").ap()
    xo = nc.dram_tensor("xo", [TOK, D], F32, kind="ExternalOutput").ap()
    x1T = nc.dram_tensor("x1T", [D, TOK], F32, kind="Internal").ap()
    with contextlib.ExitStack() as es:
        P = Prog(nc, es)
        Z = P.sb("Z", [128, DC, TM], F32)
        R1 = P.sb("R1", [128, RC, TM], BF16)
        WA = [P.sb("WA%d" % i, [128, RC, 128 * HG], BF16) for i in range(2)]
        WB = [P.sb("WB%d" % i, [128, HG, D], BF16) for i in range(2)]
        X = [P.sb("X%d" % i, [128, TM], F32) for i in range(2)]
        X1 = [P.sb("X1%d" % i, [128, TM], F32) for i in range(2)]
        RL = [P.sb("RL%d" % i, [128, TM], F32) for i in range(2)]
        HID = [P.sb("HID%d" % i, [128, HG, TM], BF16) for i in range(2)]
        T = {}
        T["SQ"] = [P.sb("SQ%d" % i, [128, TM], F32) for i in range(2)]
        for n in ("MEAN", "MSQ", "VAR", "RSTD", "NMR"):
            T[n] = P.sb(n, [128, TM], F32)
        IDF = P.sb("IDF", [128, 128], F32)
        ONESD = P.sb("ONESD", [128, 128], F32)
        MOD = P.sb("MOD", [128, 6, DC], F32)
        LNV = P.sb("LNV", [128, 4, DC], F32)
        C1 = P.sb("C1", [128, DC], F32)
        C2 = P.sb("C2", [128, DC], F32)
        S2P = P.sb("S2P", [128, DC], F32)
        PS_MM = [P.ps("PMM%d" % i, [128, TM], ids=("MM", "H", "ST")) for i in range(4)]
        PS_H = [P.ps("PH%d" % i, [128, TM]) for i in range(2)]
        PS_ST = [P.ps("PST%d" % i, [128, TM]) for i in range(2)]
        OUTV = R1[:].rearrange("p c t -> p (c t)").bitcast(F32)

        P.op("sp", lambda e: e.dma_start(out=MOD[:], in_=modT), writes=["MOD"], dma_key="MOD")
        P.op("sp", lambda e: e.dma_start(out=LNV[:], in_=lnv), writes=["LNV"], dma_key="LNV")
        P.op("pool", lambda e: e.memset(IDF[:], 0.0), writes=["IDF"])
        P.op("pool", lambda e: e.affine_select(out=IDF[:], in_=IDF[:], pattern=[[-1, 128]], compare_op=ALU.not_equal,
                                               fill=1.0, base=0, channel_multiplier=1), reads=["IDF"], writes=["IDF"])
        P.op("pool", lambda e: e.memset(ONESD[:], 1.0 / D), writes=["ONESD"])
        P.op("dve", lambda e: e.tensor_scalar(C1[:], MOD[:, 2, :], 1.0, 1.0 / ALPHA, ALU.add, ALU.mult), reads=["MOD"], writes=["C1"])
        P.op("dve", lambda e: e.tensor_scalar(C2[:], MOD[:, 5, :], 1.0, 1.0 / ALPHA, ALU.add, ALU.mult), reads=["MOD"], writes=["C2"])
        P.op("dve", lambda e: e.tensor_scalar(S2P[:], MOD[:, 4, :], 1.0, None, ALU.add), reads=["MOD"], writes=["S2P"])

        cnt = dict(wa=0, wb=0, mm=0, h=0, x=0, x1=0, rl=0, hid=0, out=0)

        def nxt(k, n):
            v = cnt[k] % n
            cnt[k] += 1
            return v

        for t in range(NT):
            tok = slice(t * TM, (t + 1) * TM)
            for k0 in range(0, KC, 8):
                P.op("sp", lambda e, tok=tok, k0=k0: e.dma_start(
                    out=R1[:, k0:k0 + 8, :], in_=aT[k0 * 128:(k0 + 8) * 128, tok].rearrange("(kc p) t -> p kc t", p=128)),
                    writes=[("R1", k) for k in range(k0, k0 + 8)], dma_key=("A", k0))
            for mg in range(DC // HG):
                sl = nxt("wa", 2)
                for k0 in range(0, KC, 8):
                    P.op("pool", lambda e, sl=sl, mg=mg, k0=k0: e.dma_start(
                        out=WA[sl][:, k0:k0 + 8, :],
                        in_=w_out[k0 * 128:(k0 + 8) * 128, mg * 128 * HG:(mg + 1) * 128 * HG].rearrange("(kc p) n -> p kc n", p=128)),
                        writes=[("WA", sl)], dma_key=("WA", sl))
                for mi in range(HG):
                    m = mg * HG + mi
                    bk = nxt("mm", 4)
                    for kc in range(KC):
                        P.op("pe", lambda e, sl=sl, mi=mi, kc=kc, bk=bk: e.matmul(
                            PS_MM[bk][:], lhsT=WA[sl][:, kc, mi * 128:(mi + 1) * 128], rhs=R1[:, kc, :],
                            start=(kc == 0), stop=(kc == KC - 1)),
                            reads=[("WA", sl), ("R1", kc)], writes=[("MM", bk)])
                    xs = nxt("x", 2)
                    P.op("sp", lambda e, xs=xs, m=m, tok=tok: e.dma_start(out=X[xs][:], in_=xT[m * 128:(m + 1) * 128, tok]),
                         writes=[("X", xs)], dma_key=("X", xs))
                    P.op("dve", lambda e, m=m, bk=bk, xs=xs: e.scalar_tensor_tensor(
                        Z[:, m, :], PS_MM[bk][:], C1[:, m:m + 1], X[xs][:], ALU.mult, ALU.add),
                        reads=[("MM", bk), ("X", xs), "C1"], writes=[("Z", m)])

            def post1(m, tok=tok):
                s = nxt("x1", 2)
                P.op("act", lambda e, m=m, s=s: e.activation(out=X1[s][:], in_=Z[:, m, :], func=AF.Identity,
                                                             bias=LNV[:, 1, m:m + 1], scale=LNV[:, 0, m:m + 1]),
                     reads=[("Z", m), "LNV"], writes=[("X1", s)])
                P.op("sp", lambda e, m=m, s=s: e.dma_start(out=x1T[m * 128:(m + 1) * 128, tok], in_=X1[s][:]),
                     reads=[("X1", s)], writes=[("x1T", m)], dma_key=("X1", s))
                P.op("act", lambda e, m=m, s=s: e.activation(out=R1[:, m, :], in_=X1[s][:], func=AF.Identity,
                                                             bias=MOD[:, 3, m:m + 1], scale=S2P[:, m:m + 1]),
                     reads=[("X1", s), "MOD", "S2P"], writes=[("R1", m)])
            ln_block(P, "ln1", Z, DC, TM, PS_ST, ONESD, T, post1)

            for g in range(HC // HG):
                sa = nxt("wa", 2)
                for k0 in range(0, DC, 8):
                    P.op("pool", lambda e, sa=sa, g=g, k0=k0: e.dma_start(
                        out=WA[sa][:, k0:k0 + 8, :],
                        in_=w_ff1[k0 * 128:(k0 + 8) * 128, g * 128 * HG:(g + 1) * 128 * HG].rearrange("(kc p) n -> p kc n", p=128)),
                        writes=[("WA", sa)], dma_key=("WA", sa))
                sb_ = nxt("wb", 2)
                P.op("pool", lambda e, sb_=sb_, g=g: e.dma_start(
                    out=WB[sb_][:], in_=w_ff2[g * 128 * HG:(g + 1) * 128 * HG, :].rearrange("(j p) n -> p j n", p=128)),
                    writes=[("WB", sb_)], dma_key=("WB", sb_))
                hs = nxt("hid", 2)
                for j in range(HG):
                    hb = nxt("h", 2)
                    for kc in range(DC):
                        P.op("pe", lambda e, sa=sa, j=j, kc=kc, hb=hb: e.matmul(
                            PS_H[hb][:], lhsT=WA[sa][:, kc, j * 128:(j + 1) * 128], rhs=R1[:, kc, :],
                            start=(kc == 0), stop=(kc == DC - 1)),
                            reads=[("WA", sa), ("R1", kc)], writes=[("H", hb)])
                    rs = nxt("rl", 2)
                    P.op("act", lambda e, hb=hb, rs=rs: e.activation(out=RL[rs][:], in_=PS_H[hb][:], func=AF.Relu),
                         reads=[("H", hb)], writes=[("RL", rs)])
                    P.op("pool", lambda e, hs=hs, j=j, rs=rs: e.tensor_tensor(HID[hs][:, j, :], RL[rs][:], RL[rs][:], ALU.mult),
                         reads=[("RL", rs)], writes=[("HID", hs, j)])
                for m in range(DC):
                    bk = nxt("mm", 4)
                    for j in range(HG):
                        P.op("pe", lambda e, sb_=sb_, j=j, m=m, bk=bk, hs=hs: e.matmul(
                            PS_MM[bk][:], lhsT=WB[sb_][:, j, m * 128:(m + 1) * 128], rhs=HID[hs][:, j, :],
                            start=(j == 0), stop=(j == HG - 1)),
                            reads=[("WB", sb_), ("HID", hs, j)], writes=[("MM", bk)])
                    if g == 0:
                        P.op("dve", lambda e, m=m, bk=bk: e.tensor_copy(Z[:, m, :], PS_MM[bk][:]),
                             reads=[("MM", bk)], writes=[("Z", m)])
                    else:
                        P.op("dve", lambda e, m=m, bk=bk: e.tensor_tensor(Z[:, m, :], Z[:, m, :], PS_MM[bk][:], ALU.add),
                             reads=[("MM", bk), ("Z", m)], writes=[("Z", m)])

            for m in range(DC):
                xs = nxt("x", 2)
                P.op("sp", lambda e, xs=xs, m=m, tok=tok: e.dma_start(out=X[xs][:], in_=x1T[m * 128:(m + 1) * 128, tok]),
                     reads=[("x1T", m)], writes=[("X", xs)], dma_key=("X", xs))
                P.op("dve", lambda e, m=m, xs=xs: e.scalar_tensor_tensor(
                    Z[:, m, :], Z[:, m, :], C2[:, m:m + 1], X[xs][:], ALU.mult, ALU.add),
                    reads=[("Z", m), ("X", xs), "C2"], writes=[("Z", m)])

            def post2(m):
                P.op("act", lambda e, m=m: e.activation(out=Z[:, m, :], in_=Z[:, m, :], func=AF.Identity,
                                                        bias=LNV[:, 3, m:m + 1], scale=LNV[:, 2, m:m + 1]),
                     reads=[("Z", m), "LNV"], writes=[("Z", m)])
            ln_block(P, "ln2", Z, DC, TM, PS_ST, ONESD, T, post2)

            for tc in range(TM // 128):
                os_ = nxt("out", 2)
                oids = [("R1", k) for k in range(os_ * (RC // 2), (os_ + 1) * (RC // 2))]
                for mq in range(DC // 4):
                    bk = nxt("mm", 4)
                    for mi in range(4):
                        m = mq * 4 + mi
                        P.op("pe", lambda e, bk=bk, mi=mi, m=m, tc=tc: e.transpose(
                            PS_MM[bk][:, mi * 128:(mi + 1) * 128], Z[:, m, tc * 128:(tc + 1) * 128], IDF[:]),
                            reads=[("Z", m), "IDF"], writes=[("MM", bk)])
                    eng = "act" if mq % 2 == 0 else "dve"
                    if eng == "act":
                        P.op("act", lambda e, bk=bk, os_=os_, mq=mq: e.copy(OUTV[:, os_ * D + mq * 512: os_ * D + (mq + 1) * 512], PS_MM[bk][:]),
                             reads=[("MM", bk)], writes=oids)
                    else:
                        P.op("dve", lambda e, bk=bk, os_=os_, mq=mq: e.tensor_copy(OUTV[:, os_ * D + mq * 512: os_ * D + (mq + 1) * 512], PS_MM[bk][:]),
                             reads=[("MM", bk)], writes=oids)
                P.op("sp", lambda e, os_=os_, t=t, tc=tc: e.dma_start(
                    out=xo[t * TM + tc * 128: t * TM + (tc + 1) * 128, :], in_=OUTV[:, os_ * D:(os_ + 1) * D]),
                    reads=oids, writes=[("xo", t, tc)], dma_key=("OUT", os_))
        P.build()
        print("ops:", {e: len(P.ops[e]) for e in P.ENGS}, "sems", P.nsem)
    return nc


HALO = 32
XH = 128
CONVW = 31


def build_front(D, NH, NG, TOK, glu, TM=512, phases=('x', 'xs', 'qkv', 'glu')):
    DC = D // 128
    QW = NH * 128
    NCOL = NG * 3 * QW + (2 * QW if glu else 0)
    CC = NH
    NT = TOK // TM
    HW = XH if glu else 0
    nc = bass.Bass("TRN2", target_bir_lowering=False)
    xin = nc.dram_tensor("xin", [XH + TOK, D], F32, kind="ExternalInput").ap()
    modT_in = nc.dram_tensor("modT_in", [128, 6, DC], F32, kind="ExternalInput").ap()
    w_in = nc.dram_tensor("w_in", [D, NCOL], F32, kind="ExternalInput").ap()
    cos_d = nc.dram_tensor("cos", [TOK, 64], F32, kind="ExternalInput").ap()
    sin_d = nc.dram_tensor("sin", [TOK, 64], F32, kind="ExternalInput").ap()
    if glu:
        convw_d = nc.dram_tensor("convw", [128, CC, CONVW], F32, kind="ExternalInput").ap()
        convv_d = nc.dram_tensor("convv", [128, 3, CC], F32, kind="ExternalInput").ap()
        flag_d = nc.dram_tensor("flag", [128, 1], F32, kind="ExternalInput").ap()
        uoT = nc.dram_tensor("uoT", [CC * 128, TOK], BF16, kind="ExternalOutput").ap()
    xT = nc.dram_tensor("xT", [D, TOK], F32, kind="ExternalOutput").ap()
    qkT = nc.dram_tensor("qkT", [NG * 2 * NH, 128, TOK], BF16, kind="ExternalOutput").ap()
    v_o = nc.dram_tensor("v", [NG, TOK, QW], BF16, kind="ExternalOutput").ap()

    with contextlib.ExitStack() as es:
        P = Prog(nc, es)
        UW = HALO + TM
        bigsz = max(3 * 128 * DC, CC * UW if glu else 0)
        BIG = P.sb("BIG", [128, bigsz], F32)
        XL = [BIG[:, i * 128 * DC:(i + 1) * 128 * DC] for i in range(2)]
        XTS = BIG[:, 2 * 128 * DC:3 * 128 * DC].rearrange("p (c t) -> p c t", t=128)
        if glu:
            U = BIG[:, :CC * UW].rearrange("p (c t) -> p c t", t=UW)
        HT = P.sb("HT", [128, DC, XH + TM], BF16)
        WS = [P.sb("WS%d" % i, [128, DC * 512], BF16) for i in range(2)]
        WB16 = [w[:].rearrange("p (k n) -> p k n", n=512) for w in WS]
        QTM = [P.sb("QTM%d" % i, [128, 4, 128], BF16) for i in range(2)]
        RT = [P.sb("RT%d" % i, [128, 4, 16], F32) for i in range(4)]
        QKS = P.sb("QKS", [128, 4, TM], BF16)
        VTM = [P.sb("VTM%d" % i, [128, 512], BF16) for i in range(2)]
        COS = P.sb("COS", [128, TM // 128, 4, 16], F32)
        SIN = P.sb("SIN", [128, TM // 128, 4, 16], F32)
        IDF = P.sb("IDF", [128, 128], F32)
        IDB = P.sb("IDB", [128, 128], BF16)
        MOD = P.sb("MOD", [128, 6, DC], F32)
        S1P = P.sb("S1P", [128, DC], F32)
        PS_MM = [P.ps("PMM%d" % i, [128, 512], ids=("MM", "TB", "PSM", "ST")) for i in range(3)]
        PS_TB = [P.ps("PTB%d" % i, [128, 512], BF16) for i in range(2)]
        PS_SM = P.ps("PSM", [128, 512])
        if glu:
            SG = [P.sb("SG%d" % i, [128, TM], F32) for i in range(2)]
            SGH = P.sb("SGH", [128, 64], F32)
            CT = [P.sb("CT%d" % i, [128, TM], F32) for i in range(2)]
            T = {}
            T["SQ"] = [P.sb("SQ%d" % i, [128, TM], F32) for i in range(2)]
            for n in ("MEAN", "MSQ", "VAR", "RSTD", "NMR"):
                T[n] = P.sb(n, [128, TM], F32)
            ONESC = P.sb("ONESC", [128, 128], F32)
            CW = P.sb("CW", [128, CC, CONVW], F32)
            CV = P.sb("CV", [128, 3, CC], F32)
            FLAG = P.sb("FLAG", [128, 1], F32)
            PS_ST = [P.ps("PST%d" % i, [128, TM]) for i in range(2)]
            UOV = HT[:].rearrange("p c t -> p (c t)")[:, :CC * TM].rearrange("p (c t) -> p c t", t=TM)

        cnt = {}

        def nxt(k, n):
            v = cnt.get(k, 0)
            cnt[k] = v + 1
            return v % n

        P.op("sp", lambda e: e.dma_start(out=MOD[:], in_=modT_in), writes=["MOD"], dma_key="c0")
        if glu:
            P.op("sp", lambda e: e.dma_start(out=CW[:], in_=convw_d), writes=["CW"], dma_key="c2")
            P.op("sp", lambda e: e.dma_start(out=CV[:], in_=convv_d), writes=["CV"], dma_key="c3")
            P.op("sp", lambda e: e.dma_start(out=FLAG[:], in_=flag_d), writes=["FLAG"], dma_key="c4")
            P.op("pool", lambda e: e.memset(ONESC[:], 1.0 / (CC * 128)), writes=["ONESC"])
        P.op("pool", lambda e: e.memset(IDF[:], 0.0), writes=["IDF"])
        P.op("pool", lambda e: e.affine_select(out=IDF[:], in_=IDF[:], pattern=[[-1, 128]], compare_op=ALU.not_equal,
                                               fill=1.0, base=0, channel_multiplier=1), reads=["IDF"], writes=["IDF"])
        P.op("pool", lambda e: e.tensor_copy(IDB[:], IDF[:]), reads=["IDF"], writes=["IDB"])
        P.op("dve", lambda e: e.tensor_scalar(S1P[:], MOD[:, 1, :], 1.0, None, ALU.add), reads=["MOD"], writes=["S1P"])

        bigids = [("XL", 0), ("XL", 1), "XTS"] + [("U", c) for c in range(CC)]

        for t in range(NT):
            r0 = XH + t * TM
            P.op("sp", lambda e, t=t: e.dma_start(out=COS[:].rearrange("p a h f -> p a (h f)"),
                                                  in_=cos_d[t * TM:(t + 1) * TM, :].rearrange("(a p) f -> p a f", p=128)),
                 writes=["COS"], dma_key="cos")
            P.op("sp", lambda e, t=t: e.dma_start(out=SIN[:].rearrange("p a h f -> p a (h f)"),
                                                  in_=sin_d[t * TM:(t + 1) * TM, :].rearrange("(a p) f -> p a f", p=128)),
                 writes=["SIN"], dma_key="sin")
            chunks = ([("h", r0 - XH, XH, 0)] if glu else []) + [("m", r0 + a * 128, 128, HW + a * 128) for a in range(TM // 128)]
            first = True
            for kind, row, nr, col in (chunks if 'x' in phases else []):
                xs = nxt("xl", 2)
                P.op("sp", lambda e, xs=xs, row=row, nr=nr: e.dma_start(out=XL[xs][:nr, :], in_=xin[row:row + nr, :]),
                     writes=(bigids if first else [("XL", xs)]), dma_key=("XL", xs))
                first = False
                for dq in range(DC // 4):
                    bk = nxt("mm", 3)
                    for i in range(4):
                        dc = dq * 4 + i
                        P.op("pe", lambda e, bk=bk, i=i, dc=dc, xs=xs, nr=nr: e.transpose(
                            PS_MM[bk][:, i * 128:i * 128 + nr], XL[xs][:nr, dc * 128:(dc + 1) * 128], IDF[:nr, :nr]),
                            reads=[("XL", xs), "IDF"], writes=[("MM", bk)])
                    for i in range(4):
                        dc = dq * 4 + i
                        P.op("act", lambda e, bk=bk, i=i, dc=dc, nr=nr, col=col: e.activation(
                            out=HT[:, dc, col:col + nr], in_=PS_MM[bk][:, i * 128:i * 128 + nr], func=AF.Identity,
                            bias=MOD[:, 0, dc:dc + 1], scale=S1P[:, dc:dc + 1]),
                            reads=[("MM", bk), "MOD", "S1P"], writes=[("HT", dc)])
                    if kind == "m" and 'xs' in phases:
                        P.op("dve", lambda e, bk=bk, dq=dq: e.tensor_copy(
                            XTS[:, dq * 4:(dq + 1) * 4, :], PS_MM[bk][:].rearrange("p (i t) -> p i t", t=128)),
                            reads=[("MM", bk)] + [("HT", dq * 4 + i) for i in range(4)], writes=["XTS"])
                if kind == "m" and 'xs' in phases:
                    a = (row - r0) // 128
                    for dc in range(DC):
                        P.op("sp", lambda e, t=t, a=a, dc=dc: e.dma_start(
                            out=xT[dc * 128:(dc + 1) * 128, t * TM + a * 128: t * TM + (a + 1) * 128], in_=XTS[:, dc, :]),
                            reads=["XTS"], writes=[("xT", t, a)], dma_key="XTS")

            for g in range(NG if 'qkv' in phases else 0):
                for part in range(3):
                    for cgi in range(QW // 512):
                        col0 = (g * 3 + part) * QW + cgi * 512
                        sl = nxt("ws", 2)
                        for k0 in range(0, DC, 8):
                            P.op("pool", lambda e, sl=sl, col0=col0, k0=k0: e.dma_start(
                                out=WB16[sl][:, k0:k0 + 8, :],
                                in_=w_in[k0 * 128:(k0 + 8) * 128, col0:col0 + 512].rearrange("(kc p) n -> p kc n", p=128)),
                                writes=[("WS", sl)], dma_key=("WS", sl))
                        for a in range(TM // 128):
                            bk = nxt("mm", 3)
                            for kc in range(DC):
                                P.op("pe", lambda e, bk=bk, kc=kc, a=a, sl=sl: e.matmul(
                                    PS_MM[bk][:], lhsT=HT[:, kc, HW + a * 128:HW + (a + 1) * 128], rhs=WB16[sl][:, kc, :],
                                    start=(kc == 0), stop=(kc == DC - 1)),
                                    reads=[("HT", kc), ("WS", sl)], writes=[("MM", bk)])
                            psv = PS_MM[bk][:].rearrange("p (h d) -> p h d", d=128)
                            if part == 2:
                                vs = nxt("vtm", 2)
                                P.op("act", lambda e, vs=vs, bk=bk: e.copy(VTM[vs][:], PS_MM[bk][:]),
                                     reads=[("MM", bk)], writes=[("VTM", vs)])
                                P.op("sp", lambda e, vs=vs, g=g, t=t, a=a, cgi=cgi: e.dma_start(
                                    out=v_o[g, t * TM + a * 128:t * TM + (a + 1) * 128, cgi * 512:(cgi + 1) * 512], in_=VTM[vs][:]),
                                    reads=[("VTM", vs)], writes=[("v", g, t, a, cgi)], dma_key=("VTM", vs))
                                continue
                            qs = nxt("qtm", 2)
                            x1 = psv[:, :, 0:16]
                            x2 = psv[:, :, 16:32]
                            cs = COS[:, a, :, :]
                            sn = SIN[:, a, :, :]
                            rd = [("MM", bk), "COS", "SIN"]
                            P.op("dve", lambda e, x1=x1, cs=cs: e.tensor_tensor(RT[0][:], x1, cs, ALU.mult), reads=rd, writes=[("RT", 0)])
                            P.op("dve", lambda e, x2=x2, sn=sn: e.tensor_tensor(RT[1][:], x2, sn, ALU.mult), reads=rd, writes=[("RT", 1)])
                            P.op("dve", lambda e, qs=qs: e.tensor_tensor(QTM[qs][:, :, 0:16], RT[0][:], RT[1][:], ALU.subtract),
                                 reads=[("RT", 0), ("RT", 1)], writes=[("QTM", qs)])
                            P.op("dve", lambda e, x2=x2, cs=cs: e.tensor_tensor(RT[2][:], x2, cs, ALU.mult), reads=rd, writes=[("RT", 2)])
                            P.op("dve", lambda e, x1=x1, sn=sn: e.tensor_tensor(RT[3][:], x1, sn, ALU.mult), reads=rd, writes=[("RT", 3)])
                            P.op("dve", lambda e, qs=qs: e.tensor_tensor(QTM[qs][:, :, 16:32], RT[2][:], RT[3][:], ALU.add),
                                 reads=[("RT", 2), ("RT", 3)], writes=[("QTM", qs)])
                            P.op("act", lambda e, qs=qs, psv=psv: e.copy(QTM[qs][:, :, 32:128], psv[:, :, 32:128]),
                                 reads=[("MM", bk)], writes=[("QTM", qs)])
                            tb = nxt("tb", 2)
                            for h in range(4):
                                P.op("pe", lambda e, tb=tb, h=h, qs=qs: e.transpose(
                                    PS_TB[tb][:, h * 128:(h + 1) * 128], QTM[qs][:, h, :], IDB[:]),
                                    reads=[("QTM", qs), "IDB"], writes=[("TB", tb)])
                            P.op("dve", lambda e, tb=tb, a=a: e.tensor_copy(
                                QKS[:, :, a * 128:(a + 1) * 128], PS_TB[tb][:].rearrange("p (h t) -> p h t", t=128)),
                                reads=[("TB", tb)], writes=["QKS"])
                        if part != 2:
                            hd0 = (g * 2 + part) * NH + cgi * 4
                            P.op("sp", lambda e, hd0=hd0, t=t: e.dma_start(
                                out=qkT[hd0:hd0 + 4, :, t * TM:(t + 1) * TM].rearrange("h d t -> d h t"), in_=QKS[:]),
                                reads=["QKS"], writes=[("qkT", hd0, t)], dma_key="QKS")

            if not glu or 'glu' not in phases:
                continue
            acol = NG * 3 * QW
            bcol = acol + QW
            for cq in range(CC // 4):
                sa = nxt("ws", 2)
                for k0 in range(0, DC, 8):
                    P.op("pool", lambda e, sa=sa, cq=cq, k0=k0: e.dma_start(
                        out=WB16[sa][:, k0:k0 + 8, :],
                        in_=w_in[k0 * 128:(k0 + 8) * 128, acol + cq * 512:acol + (cq + 1) * 512].rearrange("(kc p) n -> p kc n", p=128)),
                        writes=[("WS", sa)], dma_key=("WS", sa))
                sb_ = nxt("ws", 2)
                for k0 in range(0, DC, 8):
                    P.op("pool", lambda e, sb_=sb_, cq=cq, k0=k0: e.dma_start(
                        out=WB16[sb_][:, k0:k0 + 8, :],
                        in_=w_in[k0 * 128:(k0 + 8) * 128, bcol + cq * 512:bcol + (cq + 1) * 512].rearrange("(kc p) n -> p kc n", p=128)),
                        writes=[("WS", sb_)], dma_key=("WS", sb_))
                for i in range(4):
                    c = cq * 4 + i
                    ba = nxt("mm", 3)
                    for kc in range(DC):
                        P.op("pe", lambda e, ba=ba, kc=kc, i=i, sa=sa: e.matmul(
                            PS_MM[ba][:], lhsT=WB16[sa][:, kc, i * 128:(i + 1) * 128], rhs=HT[:, kc, XH:XH + TM],
                            start=(kc == 0), stop=(kc == DC - 1)), reads=[("HT", kc), ("WS", sa)], writes=[("MM", ba)])
                    bb = nxt("mm", 3)
                    for kc in range(DC):
                        P.op("pe", lambda e, bb=bb, kc=kc, i=i, sb_=sb_: e.matmul(
                            PS_MM[bb][:], lhsT=WB16[sb_][:, kc, i * 128:(i + 1) * 128], rhs=HT[:, kc, XH:XH + TM],
                            start=(kc == 0), stop=(kc == DC - 1)), reads=[("HT", kc), ("WS", sb_)], writes=[("MM", bb)])
                    for kc in range(DC):
                        P.op("pe", lambda e, kc=kc, i=i, sa=sa: e.matmul(
                            PS_SM[:, 0:HALO], lhsT=WB16[sa][:, kc, i * 128:(i + 1) * 128], rhs=HT[:, kc, XH - HALO:XH],
                            start=(kc == 0), stop=(kc == DC - 1)), reads=[("HT", kc), ("WS", sa)], writes=["PSM"])
                    for kc in range(DC):
                        P.op("pe", lambda e, kc=kc, i=i, sb_=sb_: e.matmul(
                            PS_SM[:, HALO:2 * HALO], lhsT=WB16[sb_][:, kc, i * 128:(i + 1) * 128], rhs=HT[:, kc, XH - HALO:XH],
                            start=(kc == 0), stop=(kc == DC - 1)), reads=[("HT", kc), ("WS", sb_)], writes=["PSM"])
                    sg = nxt("sg", 2)
                    P.op("act", lambda e, sg=sg, bb=bb: e.activation(out=SG[sg][:], in_=PS_MM[bb][:], func=AF.Sigmoid),
                         reads=[("MM", bb)], writes=[("SG", sg)])
                    wr = [("U", c)] + ([("XL", 0), ("XL", 1), "XTS"] if c == 0 else [])
                    P.op("dve", lambda e, c=c, ba=ba, sg=sg: e.tensor_tensor(U[:, c, HALO:HALO + TM], PS_MM[ba][:], SG[sg][:], ALU.mult),
                         reads=[("MM", ba), ("SG", sg)], writes=wr)
                    P.op("act", lambda e: e.activation(out=SGH[:, 0:HALO], in_=PS_SM[:, HALO:2 * HALO], func=AF.Sigmoid),
                         reads=["PSM"], writes=["SGH"])
                    P.op("dve", lambda e: e.tensor_tensor(SGH[:, HALO:2 * HALO], PS_SM[:, 0:HALO], SGH[:, 0:HALO], ALU.mult),
                         reads=["PSM", "SGH"], writes=["SGH2"])
                    if t == 0:
                        P.op("dve", lambda e, c=c: e.tensor_scalar(U[:, c, 0:HALO], SGH[:, HALO:2 * HALO], FLAG[:, 0:1], None, ALU.mult),
                             reads=["SGH2", "FLAG"], writes=[("U", c)])
                    else:
                        P.op("dve", lambda e, c=c: e.tensor_copy(U[:, c, 0:HALO], SGH[:, HALO:2 * HALO]),
                             reads=["SGH2"], writes=[("U", c)])
                    eng = "dve"
                    ct = c % 2
                    P.op(eng, lambda e, c=c, ct=ct: e.tensor_scalar(CT[ct][:], U[:, c, 2:2 + TM], CW[:, c, 0:1], CV[:, 0, c:c + 1], ALU.mult, ALU.add),
                         reads=[("U", c), "CW", "CV"], writes=[("CT", ct)])
                    for j in range(1, CONVW - 1):
                        P.op(eng, lambda e, c=c, ct=ct, j=j: e.scalar_tensor_tensor(
                            CT[ct][:], U[:, c, 2 + j:2 + j + TM], CW[:, c, j:j + 1], CT[ct][:], ALU.mult, ALU.add),
                            reads=[("U", c), "CW", ("CT", ct)], writes=[("CT", ct)])
                    j = CONVW - 1
                    P.op(eng, lambda e, c=c, ct=ct, j=j: e.scalar_tensor_tensor(
                        U[:, c, HALO:HALO + TM], U[:, c, HALO:HALO + TM], CW[:, c, j:j + 1], CT[ct][:], ALU.mult, ALU.add),
                        reads=[("U", c), "CW", ("CT", ct)], writes=[("U", c)])

            class ZV:
                def __getitem__(self, key):
                    return U[key[0], key[1], HALO:HALO + TM]

            def postc(m):
                P.op("act", lambda e, m=m: e.activation(out=UOV[:, m, :], in_=U[:, m, HALO:HALO + TM], func=AF.Silu,
                                                        bias=CV[:, 2, m:m + 1], scale=CV[:, 1, m:m + 1]),
                     reads=[("U", m), "CV"] , writes=[("HT", k) for k in range(DC)])
            ln_block(P, "lnc", ZV(), CC, TM, PS_ST, ONESC, T, postc, eps=1e-5, zname="U")
            for c0 in range(0, CC, 8):
                P.op("sp", lambda e, t=t, c0=c0: e.dma_start(
                    out=uoT[c0 * 128:(c0 + 8) * 128, t * TM:(t + 1) * TM].rearrange("(c p) t -> p c t", p=128), in_=UOV[:, c0:c0 + 8, :]),
                    reads=[("HT", k) for k in range(DC)], writes=[("uoT", t, c0)], dma_key="UO")
        P.build()
        print("front ops:", {e: len(P.ops[e]) for e in P.ENGS}, "sems", P.nsem)
    return nc


NEGV = -30000.0
MOBA_BLOCK = 256
DIL_CONFIGS = ((128, 1), (512, 4), (2048, 16))


def moba_consts():
    E = np.zeros((64, 64, 128), np.float32)
    for n in range(64):
        E[n, n, :] = 1.0
    dm = np.zeros((128, 4, 512), np.float32)
    k = np.arange(128)[:, None]
    q = np.arange(512)[None, :]
    for j in range(4):
        kk = k + 128 * j
        same = (kk // 256) == (q // 256)
        ok = same & (kk <= q)
        if j < 2:
            ok = ok | (q >= 256)
        dm[:, j, :] = np.where(ok, 0.0, NEGV)
    return E.astype(ml_dtypes.bfloat16), dm.astype(ml_dtypes.bfloat16)


def dil_masks():
    tiles = []
    index = {}
    k = np.arange(128)[:, None]
    q = np.arange(512)[None, :]
    for ci, (w, dil) in enumerate(DIL_CONFIGS):
        for o in range(-w, 512, 128):
            d = (q - k) - o
            ok = (d >= 0) & (d <= w) & (d % dil == 0)
            index[(ci, o)] = len(tiles)
            tiles.append(np.where(ok, 0.0, NEGV))
    return np.stack(tiles, 1).astype(ml_dtypes.bfloat16), index


def build_attn(kind, S, NHC=2):
    NG = 1 if kind == "moba" else 3
    NQG = S // 512
    NKT = S // 128
    scale = 128 ** -0.5
    nc = bass.Bass("TRN2", target_bir_lowering=False)
    QT = nc.dram_tensor("QT", [NG, NHC, 128, S], BF16, kind="ExternalInput").ap()
    KT = nc.dram_tensor("KT", [NG, NHC, 128, S], BF16, kind="ExternalInput").ap()
    Vd = nc.dram_tensor("V", [NG, S, NHC * 128], BF16, kind="ExternalInput").ap()
    OT = nc.dram_tensor("OT", [NHC * 128, S], BF16, kind="ExternalOutput").ap()
    if kind == "moba":
        E_d = nc.dram_tensor("E", [64, 64, 128], BF16, kind="ExternalInput").ap()
        DM_d = nc.dram_tensor("DM", [128, 4, 512], BF16, kind="ExternalInput").ap()
        NM = 4
    else:
        _, midx = dil_masks()
        NM = len(midx)
        DM_d = nc.dram_tensor("DM", [128, NM, 512], BF16, kind="ExternalInput").ap()

    with contextlib.ExitStack() as es:
        P = Prog(nc, es)
        DM = P.sb("DMs", [128, NM, 512], BF16)
        IDF = P.sb("IDF", [128, 128], F32)
        IDB = P.sb("IDB", [128, 128], BF16)
        ONESB = P.sb("ONESB", [128, 128], BF16)
        QG = [[P.sb("QG%d_%d" % (g, i), [128, 512], BF16) for i in range(2)] for g in range(NG)]
        PT = [P.sb("PT%d" % i, [128, 512], BF16) for i in range(3)]
        RL = P.sb("RL", [128, 512], F32)
        OS = [P.sb("OS%d" % i, [128, 512], BF16) for i in range(2)]
        PS_S = [P.ps("PS_S%d" % i, [128, 512], ids=("S", "O", "L", "G", "T")) for i in range(2)]
        PS_O = [P.ps("PS_O%d" % i, [128, 512]) for i in range(2)]
        PS_L = [P.ps("PS_L%d" % i, [128, 512]) for i in range(2)]
        if kind == "moba":
            KTS = P.sb("KTS", [128, S], BF16)
            VS = P.sb("VS", [128, NKT, 128], BF16)
            Eb = P.sb("Eb", [64, 64, 128], BF16)
            KM = P.sb("KM", [128, 64], F32)
            KMB = P.sb("KMB", [128, 64], BF16)
            GSB = P.sb("GSB", [128, 64], F32)
            MX = P.sb("MX", [128, 8], F32)
            NEG = P.sb("NEG", [128, 4, 64], BF16)
            NEGT = [P.sb("NEGT%d" % i, [64, 512], BF16) for i in range(2)]
            PS_G = P.ps("PS_G", [128, 512])
            PS_T = P.ps("PS_T", [128, 512], BF16)
        else:
            NS = [w // 512 + 2 + (1 if w % 512 else 0) for (w, d) in DIL_CONFIGS]
            NS = [max(n, 3) for n in NS]
            KR = [[P.sb("KR%d_%d" % (c, s), [128, 512], BF16) for s in range(NS[c])] for c in range(3)]
            VR = [[P.sb("VR%d_%d" % (c, s), [128, 4, 128], BF16) for s in range(NS[c])] for c in range(3)]

        cnt = {}

        def nxt(k, n):
            v = cnt.get(k, 0)
            cnt[k] = v + 1
            return v % n

        P.op("sp", lambda e: e.dma_start(out=DM[:], in_=DM_d), writes=["DM"], dma_key="c0")
        P.op("pool", lambda e: e.memset(IDF[:], 0.0), writes=["IDF"])
        P.op("pool", lambda e: e.affine_select(out=IDF[:], in_=IDF[:], pattern=[[-1, 128]], compare_op=ALU.not_equal,
                                               fill=1.0, base=0, channel_multiplier=1), reads=["IDF"], writes=["IDF"])
        P.op("pool", lambda e: e.tensor_copy(IDB[:], IDF[:]), reads=["IDF"], writes=["IDB"])
        P.op("pool", lambda e: e.memset(ONESB[:], 1.0), writes=["ONESB"])
        if kind == "moba":
            P.op("sp", lambda e: e.dma_start(out=Eb[:], in_=E_d), writes=["E"], dma_key="c1")
            P.op("pool", lambda e: e.memset(KMB[:], 0.0), writes=["KMB"])

        def load_q(h, g):
            for gi in range(NG):
                sl = g % 2
                P.op("sp", lambda e, gi=gi, sl=sl, h=h, g=g: e.dma_start(out=QG[gi][sl][:], in_=QT[gi, h, :, g * 512:(g + 1) * 512]),
                     writes=[("QG", gi, sl)], dma_key=("QG", gi, sl))

        def finish(h, g, ob):
            P.op("dve", lambda e, ob=ob: e.reciprocal(RL[:], PS_L[ob][:]), reads=[("L", ob)], writes=["RL"])
            osl = nxt("os", 2)
            P.op("dve", lambda e, ob=ob, osl=osl: e.tensor_tensor(OS[osl][:], PS_O[ob][:], RL[:], ALU.mult),
                 reads=[("O", ob), "RL"], writes=[("OS", osl)])
            P.op("sp", lambda e, osl=osl, h=h, g=g: e.dma_start(out=OT[h * 128:(h + 1) * 128, g * 512:(g + 1) * 512], in_=OS[osl][:]),
                 reads=[("OS", osl)], writes=[("OT", h, g)], dma_key=("OS", osl))

        def tile_pipeline(tiles, ob, inject=None):
            n = len(tiles)
            sb_of = {}

            def emit_S(i):
                sbk = nxt("sbank", 2)
                sb_of[i] = sbk
                ops = tiles[i]["s_ops"]
                for j, (lf, rf, rd) in enumerate(ops):
                    P.op("pe", lambda e, lf=lf, rf=rf, j=j, sbk=sbk, last=(j == len(ops) - 1): e.matmul(
                        PS_S[sbk][:], lhsT=lf(), rhs=rf(), start=(j == 0), stop=last),
                        reads=rd, writes=[("S", sbk)])
            emit_S(0)
            for i in range(n):
                if i + 1 < n:
                    emit_S(i + 1)
                sbk = sb_of[i]
                pt = nxt("pt", 3)
                P.op("act", lambda e, sbk=sbk, pt=pt: e.activation(out=PT[pt][:], in_=PS_S[sbk][:], func=AF.Exp, scale=scale),
                     reads=[("S", sbk)], writes=[("PT", pt)])
                vf = tiles[i]["v_fn"]
                P.op("pe", lambda e, vf=vf, pt=pt, i=i, ob=ob: e.matmul(PS_O[ob][:], lhsT=vf(), rhs=PT[pt][:], start=(i == 0), stop=(i == n - 1)),
                     reads=tiles[i]["v_reads"] + [("PT", pt)], writes=[("O", ob)])
                P.op("pe", lambda e, pt=pt, i=i, ob=ob: e.matmul(PS_L[ob][:], lhsT=ONESB[:], rhs=PT[pt][:], start=(i == 0), stop=(i == n - 1)),
                     reads=["ONESB", ("PT", pt)], writes=[("L", ob)])
                if inject is not None and i == min(1, n - 1):
                    inject()

        if kind == "moba":
            def prep_gate(h, g):
                sl = g % 2
                for qt in range(4):
                    P.op("pe", lambda e, qt=qt, sl=sl: e.matmul(PS_G[:, qt * 64:(qt + 1) * 64], lhsT=QG[0][sl][:, qt * 128:(qt + 1) * 128],
                                                                 rhs=KMB[:], start=True, stop=True),
                         reads=[("QG", 0, sl), "KMB"], writes=["G"])
                for qt in range(4):
                    b = 2 * g + qt // 2
                    if b == 0:
                        continue
                    P.op("dve", lambda e, qt=qt, b=b: e.tensor_copy(GSB[:, 0:b], PS_G[:, qt * 64:qt * 64 + b]),
                         reads=["G"], writes=["GSB"])
                    P.op("dve", lambda e: e.max(MX[:], GSB[:]), reads=["GSB"], writes=["MX"])
                    P.op("dve", lambda e, qt=qt, b=b: e.tensor_scalar(NEG[:, qt, 0:b], GSB[:, 0:b], MX[:, 2:3], NEGV, ALU.is_lt, ALU.mult),
                         reads=["GSB", "MX"], writes=[("NEG", qt)])

            def prep_T(h, g):
                sl = g % 2
                for qt in range(4):
                    P.op("pe", lambda e, qt=qt: e.transpose(PS_T[:64, qt * 128:(qt + 1) * 128], NEG[:, qt, :], IDB[:]),
                         reads=[("NEG", qt), "IDB"], writes=["T"])
                P.op("act", lambda e, sl=sl: e.copy(NEGT[sl][:], PS_T[:64, :]), reads=["T"], writes=[("NEGT", sl)])

            for h in range(NHC):
                P.op("sp", lambda e, h=h: e.dma_start(out=KTS[:], in_=KT[0, h, :, :]), writes=["KTS"], dma_key="KTS")
                for c0 in range(0, NKT, 8):
                    P.op("pool", lambda e, h=h, c0=c0: e.dma_start(
                        out=VS[:, c0:c0 + 8, :], in_=Vd[0, c0 * 128:(c0 + 8) * 128, h * 128:(h + 1) * 128].rearrange("(c p) d -> p c d", p=128)),
                        writes=["VS"], dma_key="VS")
                P.op("dve", lambda e: e.tensor_reduce(KM[:, :S // MOBA_BLOCK], KTS[:].rearrange("p (n k) -> p n k", k=MOBA_BLOCK), mybir.AxisListType.X, ALU.add),
                     reads=["KTS"], writes=["KM"])
                P.op("dve", lambda e: e.tensor_scalar(KMB[:, :S // MOBA_BLOCK], KM[:, :S // MOBA_BLOCK], 1.0 / MOBA_BLOCK, None, ALU.mult), reads=["KM"], writes=["KMB"])
                P.op("pool", lambda e: e.memset(GSB[:], -1e30), writes=["GSB"])
                P.op("pool", lambda e: e.memset(NEG[:], 0.0), writes=[("NEG", q) for q in range(4)])
                load_q(h, 0)
                prep_gate(h, 0)
                prep_T(h, 0)
                for g in range(NQG):
                    sl = g % 2
                    if g + 1 < NQG:
                        load_q(h, g + 1)
                        prep_gate(h, g + 1)
                    tiles = []
                    for kt in range(4 * g + 4):
                        n = kt // 2
                        s_ops = [(lambda kt=kt: KTS[:, kt * 128:(kt + 1) * 128], lambda sl=sl: QG[0][sl][:], ["KTS", ("QG", 0, sl)])]
                        if n <= 2 * g:
                            s_ops.append((lambda n=n: Eb[:, n, :], lambda sl=sl: NEGT[sl][:], ["E", ("NEGT", sl)]))
                        if n >= 2 * g:
                            j = kt - 4 * g
                            s_ops.append((lambda: IDB[:], lambda j=j: DM[:, j, :], ["IDB", "DM"]))
                        tiles.append(dict(s_ops=s_ops, v_fn=(lambda kt=kt: VS[:, kt, :]), v_reads=["VS"]))
                    ob = nxt("ob", 2)
                    inj = (lambda h=h, g=g: prep_T(h, g + 1)) if g + 1 < NQG else None
                    tile_pipeline(tiles, ob, inj)
                    finish(h, g, ob)
        else:
            def load_slab(h, g):
                for c in range(3):
                    s = g % NS[c]
                    P.op("sp", lambda e, c=c, s=s, h=h, g=g: e.dma_start(out=KR[c][s][:], in_=KT[c, h, :, g * 512:(g + 1) * 512]),
                         writes=[("K", c, s)], dma_key=("K", c, s))
                    P.op("pool", lambda e, c=c, s=s, h=h, g=g: e.dma_start(
                        out=VR[c][s][:], in_=Vd[c, g * 512:(g + 1) * 512, h * 128:(h + 1) * 128].rearrange("(a p) d -> p a d", p=128)),
                        writes=[("V", c, s)], dma_key=("V", c, s))
            for h in range(NHC):
                load_q(h, 0)
                load_slab(h, 0)
                for g in range(NQG):
                    sl = g % 2
                    if g + 1 < NQG:
                        load_q(h, g + 1)
                        load_slab(h, g + 1)
                    tiles = []
                    for c, (w, dil) in enumerate(DIL_CONFIGS):
                        for o in range(-w, 512, 128):
                            k0 = 512 * g + o
                            if k0 < 0:
                                continue
                            slab, a = divmod(k0, 512)
                            s = slab % NS[c]
                            a = a // 128
                            mi = midx[(c, o)]
                            s_ops = [(lambda c=c, s=s, a=a: KR[c][s][:, a * 128:(a + 1) * 128], lambda c=c, sl=sl: QG[c][sl][:],
                                      [("K", c, s), ("QG", c, sl)]),
                                     (lambda: IDB[:], lambda mi=mi: DM[:, mi, :], ["IDB", "DM"])]
                            tiles.append(dict(s_ops=s_ops, v_fn=(lambda c=c, s=s, a=a: VR[c][s][:, a, :]), v_reads=[("V", c, s)]))
                    ob = nxt("ob", 2)
                    tile_pipeline(tiles, ob)
                    finish(h, g, ob)
        P.build()
        print(kind, "ops:", {e: len(P.ops[e]) for e in P.ENGS}, "sems", P.nsem)
    return nc


def build_mod(D, CWID):
    DC = D // 128
    NJ = CWID // 128
    nc = bass.Bass("TRN2", target_bir_lowering=False)
    condT = nc.dram_tensor("condT", [128, DC], F32, kind="ExternalInput").ap()
    wpart = nc.dram_tensor("wpart", [2, D, CWID], F32, kind="ExternalInput").ap()
    bpart = nc.dram_tensor("bpart", [128, 2, NJ], F32, kind="ExternalInput").ap()
    modp = nc.dram_tensor("modp", [128, 2, NJ], F32, kind="ExternalOutput").ap()
    with contextlib.ExitStack() as es:
        P = Prog(nc, es)
        WS = [P.sb("WS%d" % i, [128, DC, 256], F32) for i in range(2)]
        SC = P.sb("SC", [128, DC], F32)
        BP = P.sb("BP", [128, 2, NJ], F32)
        MO = P.sb("MO", [128, 2, NJ], F32)
        PS = P.ps("PS", [128, 512], ids=("PS",))
        P.op("sp", lambda e: e.dma_start(out=SC[:], in_=condT), writes=["SC"], dma_key="c0")
        P.op("sp", lambda e: e.dma_start(out=BP[:], in_=bpart), writes=["BP"], dma_key="c1")
        P.op("act", lambda e: e.activation(out=SC[:], in_=SC[:], func=AF.Silu), reads=["SC"], writes=["SC"])
        k = 0
        for i in range(2):
            for cg in range(CWID // 256):
                sl = k % 2
                k += 1
                for k0 in range(0, DC, 8):
                    P.op("sp", lambda e, sl=sl, i=i, cg=cg, k0=k0: e.dma_start(
                        out=WS[sl][:, k0:k0 + 8, :],
                        in_=wpart[i, k0 * 128:(k0 + 8) * 128, cg * 256:(cg + 1) * 256].rearrange("(kc p) n -> p kc n", p=128)),
                        writes=[("WS", sl)], dma_key=("WS", sl))
                for ci in range(2):
                    col = i * NJ + cg * 2 + ci
                    for kc in range(DC):
                        P.op("pe", lambda e, sl=sl, ci=ci, col=col, kc=kc: e.matmul(
                            PS[:, col:col + 1], lhsT=WS[sl][:, kc, ci * 128:(ci + 1) * 128], rhs=SC[:, kc:kc + 1],
                            start=(kc == 0), stop=(kc == DC - 1)),
                            reads=[("WS", sl), "SC"], writes=["PS"])
        P.op("dve", lambda e: e.tensor_tensor(MO[:].rearrange("p i j -> p (i j)"), PS[:, :2 * NJ],
                                              BP[:].rearrange("p i j -> p (i j)"), ALU.add),
             reads=["PS", "BP"], writes=["MO"])
        P.op("sp", lambda e: e.dma_start(out=modp, in_=MO[:]), reads=["MO"], writes=["modp"], dma_key="c2")
        P.build()
    return nc


D_MODEL = 4096
SEQ = 16384
NCORE = 8
TOK = SEQ // NCORE
NHEADS = 16
D_FF = 4 * D_MODEL
_NC_CACHE = {}
_DEBUG = False


def _get(name, fn):
    if name not in _NC_CACHE:
        _NC_CACHE[name] = fn()
    return _NC_CACHE[name]


def _run(nc, in_maps):
    import time as _time
    t0 = _time.time()
    res = run_bass_kernel_spmd(nc, in_maps, core_ids=list(range(NCORE)))
    print("launch wall %.1fs" % (_time.time() - t0), flush=True)
    if _DEBUG:
        for cc, r in enumerate(res.results):
            for k, v in r.items():
                a = np.asarray(v).astype(np.float32)
                bad = ~np.isfinite(a)
                if bad.any():
                    idx = np.argwhere(bad)
                    print("NONFINITE core", cc, k, a.shape, "count", int(bad.sum()), "first", idx[:3].tolist(), "last", idx[-3:].tolist(), flush=True)
                else:
                    print("finite core", cc, k, "absmax %.4g" % np.abs(a).max(), flush=True)
    return res.results


def vec_layout(v, DC):
    n = v.shape[0]
    return np.ascontiguousarray(np.asarray(v, np.float32).reshape(n, DC, 128).transpose(2, 0, 1))


def rope_tables(pos):
    inv = (np.float32(500000.0) ** (-np.arange(0, 32, 2, dtype=np.float32) / np.float32(32))).astype(np.float32)
    ang = pos.astype(np.float32)[:, None] * inv[None, :]
    return np.cos(ang).astype(np.float32), np.sin(ang).astype(np.float32)


def _attn_inputs(fr, NG):
    ins = []
    for hp in range(NCORE):
        QT = np.empty((NG, 2, 128, SEQ), BF)
        KT = np.empty((NG, 2, 128, SEQ), BF)
        V = np.empty((NG, SEQ, 256), BF)
        for cc in range(NCORE):
            qk = np.asarray(fr[cc]["qkT"])
            vv = np.asarray(fr[cc]["v"])
            for g in range(NG):
                for hh in range(2):
                    QT[g, hh, :, cc * TOK:(cc + 1) * TOK] = qk[(g * 2 + 0) * NHEADS + 2 * hp + hh]
                    KT[g, hh, :, cc * TOK:(cc + 1) * TOK] = qk[(g * 2 + 1) * NHEADS + 2 * hp + hh]
                V[g, cc * TOK:(cc + 1) * TOK, :] = vv[g, :, hp * 256:(hp + 1) * 256]
        ins.append(dict(QT=QT, KT=KT, V=V))
    return ins


def _back_aT(att, extra, cc):
    parts = [np.asarray(att[hp]["OT"])[:, cc * TOK:(cc + 1) * TOK] for hp in range(NCORE)]
    if extra is not None:
        parts.append(np.asarray(extra[cc]["uoT"]))
    return np.ascontiguousarray(np.concatenate(parts, axis=0))


def kernel(x, c, w_ada, b_ada, w_in_ab, w_out_ab, conv_w, conv_b, conv_ln_g, conv_ln_b,
           w_in_c, w_out_c, w_ff1, w_ff2, ln_g, ln_b):
    D, DC = D_MODEL, D_MODEL // 128
    x2d = np.asarray(x, np.float32).reshape(SEQ, D)
    c = np.asarray(c, np.float32).reshape(D)
    w_ada = np.asarray(w_ada, np.float32)
    b_ada = np.asarray(b_ada, np.float32)
    condT = np.ascontiguousarray(c.reshape(DC, 128).T)

    CWID = 6 * D // NCORE
    ncm = _get("mod", lambda: build_mod(D, CWID))
    ins = []
    for cc in range(NCORE):
        cols = slice(cc * CWID, (cc + 1) * CWID)
        bp = b_ada[:, cols].reshape(2, CWID // 128, 128).transpose(2, 0, 1)
        ins.append(dict(condT=condT, wpart=np.ascontiguousarray(w_ada[:, :, cols]), bpart=np.ascontiguousarray(bp)))
    rm = _run(ncm, ins)
    modflat = np.concatenate([np.asarray(rm[cc]["modp"]).transpose(1, 2, 0).reshape(2, CWID) for cc in range(NCORE)], axis=1)
    modT = [vec_layout(modflat[i].reshape(6, D), DC) for i in range(2)]
    lnv = [vec_layout(np.stack([ln_g[i, 0], ln_b[i, 0], ln_g[i, 1], ln_b[i, 1]]), DC) for i in range(2)]

    def front_inputs(xcur, layer, w_in):
        ins = []
        for cc in range(NCORE):
            xin = np.zeros((XH + TOK, D), np.float32)
            lo = cc * TOK - XH
            if lo >= 0:
                xin[:] = xcur[lo:(cc + 1) * TOK]
            else:
                xin[XH:] = xcur[:TOK]
            cos, sin = rope_tables(np.arange(cc * TOK, (cc + 1) * TOK))
            d = dict(xin=xin, modT_in=modT[layer], w_in=w_in, cos=np.tile(cos, (1, 4)), sin=np.tile(sin, (1, 4)))
            if layer == 0:
                d["convw"] = np.ascontiguousarray(np.asarray(conv_w[0], np.float32).reshape(CONVW, NHEADS, 128).transpose(2, 1, 0))
                d["convv"] = vec_layout(np.stack([conv_b[0], conv_ln_g[0], conv_ln_b[0]]), NHEADS)
                d["flag"] = np.full((128, 1), 0.0 if cc == 0 else 1.0, np.float32)
            ins.append(d)
        return ins

    def back_inputs(att, fr, layer, w_out, extra):
        ins = []
        for cc in range(NCORE):
            ins.append(dict(aT=_back_aT(att, extra, cc), xT=np.asarray(fr[cc]["xT"]), modT=modT[layer], lnv=lnv[layer],
                            w_out=w_out, w_ff1=np.asarray(w_ff1[layer], np.float32), w_ff2=np.asarray(w_ff2[layer], np.float32)))
        return ins

    nf0 = _get("front0", lambda: build_front(D, NHEADS, 1, TOK, True))
    fr0 = _run(nf0, front_inputs(x2d, 0, np.asarray(w_in_ab[0], np.float32)))
    na0 = _get("moba", lambda: build_attn("moba", SEQ))
    E, dm = moba_consts()
    ai = _attn_inputs(fr0, 1)
    for d in ai:
        d["E"] = E
        d["DM"] = dm
    at0 = _run(na0, ai)
    nb0 = _get("back0", lambda: build_back(D, D_FF, 2 * NHEADS * 128, TOK))
    bk0 = _run(nb0, back_inputs(at0, fr0, 0, np.asarray(w_out_ab[0], np.float32), fr0))
    x1 = np.concatenate([np.asarray(bk0[cc]["xo"]) for cc in range(NCORE)], axis=0)

    nf1 = _get("front1", lambda: build_front(D, NHEADS, 3, TOK, False))
    fr1 = _run(nf1, front_inputs(x1, 1, np.asarray(w_in_c[0], np.float32)))
    na1 = _get("dil", lambda: build_attn("dil", SEQ))
    dmask = dil_masks()[0]
    ai = _attn_inputs(fr1, 3)
    for d in ai:
        d["DM"] = dmask
    at1 = _run(na1, ai)
    nb1 = _get("back1", lambda: build_back(D, D_FF, NHEADS * 128, TOK))
    bk1 = _run(nb1, back_inputs(at1, fr1, 1, np.asarray(w_out_c[0], np.float32), None))
    out = np.concatenate([np.asarray(bk1[cc]["xo"]) for cc in range(NCORE)], axis=0)
    return out.reshape(1, SEQ, D).astype(np.float32)
```
